# Optimizing a Trainium2 kernel written in Bass

```python
import math
import jax
import jax.numpy as jnp
from jax import lax
import numpy as np

D_MODEL = 2048
BATCH = 8
SEQ = 4096
DEPTH = 2

GRID_W = 64
CTX_LEN = 256
HEAD_DIM = 128
GROUP_WIDTH = 512
N_GROUPS = 4
MIX_WIDTH = N_GROUPS * GROUP_WIDTH
GQA_HEADS = 4
GQA_KV_HEADS = 2
DIFF_HEADS = 4
DIFF_QK_DIM = 64
DIFF_V_DIM = 128
WIN_HEADS = 4
WIN_KV_HEADS = 2
WINDOW = 128
MLA_HEADS = 4
MLA_Q_RANK = 512
MLA_KV_RANK = 256
MLA_NOPE = 128
MLA_ROPE = 64
MLA_V = 128
MLA_QK_DIM = MLA_NOPE + MLA_ROPE
D_FF = ((8 * D_MODEL + 3 * 256 - 1) // (3 * 256)) * 256
Q_BLOCK = 128
N_NBR = (WINDOW + Q_BLOCK - 1) // Q_BLOCK
BAND = (2 * N_NBR + 1) * Q_BLOCK
ROPE_BASE = 10000.0
NORM_EPS = 1e-6
NEG_INF = -1e30
IN_SPLITS = (
    GQA_HEADS * HEAD_DIM, GQA_KV_HEADS * HEAD_DIM, GQA_KV_HEADS * HEAD_DIM,
    DIFF_HEADS * 2 * DIFF_QK_DIM, DIFF_HEADS * 2 * DIFF_QK_DIM, DIFF_HEADS * DIFF_V_DIM,
    WIN_HEADS * HEAD_DIM, WIN_KV_HEADS * HEAD_DIM, WIN_KV_HEADS * HEAD_DIM,
    MLA_Q_RANK, MLA_KV_RANK, MLA_ROPE,
)
IN_COLS = sum(IN_SPLITS)

kernel_name = 'hybrid_parallel_heads_flow_block'


def rmsnorm(x, g):
    xf = x.astype(jnp.float32)
    y = xf * lax.rsqrt(jnp.mean(xf * xf, axis=-1, keepdims=True) + NORM_EPS)
    return (y * g.astype(jnp.float32)).astype(x.dtype)


def modulate(h, shift, scale):
    return h * (1.0 + scale) + shift


def softmax_f32(s):
    return jax.nn.softmax(s.astype(jnp.float32), axis=-1)


def axial_rope_tables(n_tokens, dim):
    rows = n_tokens // GRID_W
    row = jnp.broadcast_to(jnp.arange(rows)[:, None], (rows, GRID_W)).reshape(-1).astype(jnp.float32)
    col = jnp.broadcast_to(jnp.arange(GRID_W)[None, :], (rows, GRID_W)).reshape(-1).astype(jnp.float32)
    quarter = dim // 4
    inv_freq = ROPE_BASE ** (-jnp.arange(quarter, dtype=jnp.float32) / quarter)
    ang = jnp.concatenate([row[:, None] * inv_freq, col[:, None] * inv_freq], axis=-1)
    return jnp.cos(ang), jnp.sin(ang)


def apply_rope(x, rope):
    cos, sin = rope
    shape = (cos.shape[0],) + (1,) * (x.ndim - 3) + (cos.shape[1],)
    cos, sin = cos.reshape(shape), sin.reshape(shape)
    xf = x.astype(jnp.float32)
    half = x.shape[-1] // 2
    x1, x2 = xf[..., :half], xf[..., half:]
    return jnp.concatenate([x1 * cos - x2 * sin, x2 * cos + x1 * sin], axis=-1).astype(x.dtype)


def split_columns(y):
    offsets = []
    acc = 0
    for w in IN_SPLITS[:-1]:
        acc += w
        offsets.append(acc)
    return jnp.split(y, offsets, axis=-1)


def to_blocks(a):
    b, t = a.shape[:2]
    return jnp.moveaxis(a.reshape(b, t // Q_BLOCK, Q_BLOCK, *a.shape[2:]), 1, 0)


def from_blocks(a):
    nb, b = a.shape[:2]
    return jnp.moveaxis(a, 0, 1).reshape(b, nb * Q_BLOCK, *a.shape[3:])


def sweep_query_blocks(block_fn, *qs):
    return from_blocks(lax.map(block_fn, tuple(to_blocks(q) for q in qs)))


def dense_gqa(q, k, v, scale):
    def block(args):
        (qb,) = args
        s = jnp.einsum('bqgrd,bkgd->bgrqk', qb, k) * scale
        p = softmax_f32(s).astype(v.dtype)
        return jnp.einsum('bgrqk,bkge->bqgre', p, v)
    o = sweep_query_blocks(block, q)
    return o.reshape(o.shape[0], o.shape[1], -1)


def diff_attention(q1, q2, k1, k2, v, lam, scale):
    def block(args):
        q1b, q2b = args
        p1 = softmax_f32(jnp.einsum('bqhd,bkhd->bhqk', q1b, k1) * scale)
        p2 = softmax_f32(jnp.einsum('bqhd,bkhd->bhqk', q2b, k2) * scale)
        return jnp.einsum('bhqk,bkhe->bqhe', (p1 - lam * p2).astype(v.dtype), v)
    return sweep_query_blocks(block, q1, q2)


def sink_attention_block(qb, k, v, sink, scale, mask):
    s = jnp.einsum('bqgrd,bkgd->bgrqk', qb, k).astype(jnp.float32) * scale
    if mask is not None:
        s = jnp.where(mask, s, NEG_INF)
    sink_col = jnp.broadcast_to(sink.astype(jnp.float32)[None, :, :, None, None], s.shape[:-1] + (1,))
    p = softmax_f32(jnp.concatenate([s, sink_col], axis=-1))[..., :-1]
    return jnp.einsum('bgrqk,bkge->bqgre', p.astype(v.dtype), v)


def windowed_sink_attention(q, k, v, k_ctx, v_ctx, sink, scale):
    b, t = q.shape[:2]
    nb = t // Q_BLOCK
    pad = N_NBR * Q_BLOCK

    def band(a):
        ap = jnp.pad(a, ((0, 0), (pad, pad), (0, 0), (0, 0)))
        ab = ap.reshape(b, nb + 2 * N_NBR, Q_BLOCK, *a.shape[2:])
        banded = jnp.concatenate([ab[:, j:j + nb] for j in range(2 * N_NBR + 1)], axis=2)
        return jnp.moveaxis(banded, 1, 0)

    blk = jnp.arange(nb)[:, None, None]
    q_pos = blk * Q_BLOCK + jnp.arange(Q_BLOCK)[None, :, None]
    k_pos = (blk - N_NBR) * Q_BLOCK + jnp.arange(BAND)[None, None, :]
    band_mask = (jnp.abs(q_pos - k_pos) <= WINDOW) & (k_pos >= 0) & (k_pos < t)
    ctx_mask = jnp.ones((Q_BLOCK, k_ctx.shape[1]), dtype=bool)

    def block(args):
        qb, kb, vb, mb = args
        keys = jnp.concatenate([k_ctx, kb], axis=1)
        vals = jnp.concatenate([v_ctx, vb], axis=1)
        return sink_attention_block(qb, keys, vals, sink, scale, jnp.concatenate([ctx_mask, mb], axis=-1))

    o = from_blocks(lax.map(block, (to_blocks(q), band(k), band(v), band_mask)))
    return o.reshape(b, t, -1)


def mixer_gqa(q, k, v, qc, kc, vc, q_gain, k_gain, rope, need_ctx_out):
    rep = GQA_HEADS // GQA_KV_HEADS
    scale = HEAD_DIM ** -0.5

    def prep(q, k, v, rotate):
        b, t = q.shape[:2]
        q = rmsnorm(q.reshape(b, t, GQA_HEADS, HEAD_DIM), q_gain)
        k = rmsnorm(k.reshape(b, t, GQA_KV_HEADS, HEAD_DIM), k_gain)
        if rotate:
            q, k = apply_rope(q, rope), apply_rope(k, rope)
        return q.reshape(b, t, GQA_KV_HEADS, rep, HEAD_DIM), k, v.reshape(b, t, GQA_KV_HEADS, HEAD_DIM)

    q, k, v = prep(q, k, v, True)
    qc, kc, vc = prep(qc, kc, vc, False)
    out = dense_gqa(q, jnp.concatenate([kc, k], axis=1), jnp.concatenate([vc, v], axis=1), scale)
    out_c = dense_gqa(qc, kc, vc, scale) if need_ctx_out else None
    return out, out_c


def mixer_diff(q, k, v, qc, kc, vc, lq1, lk1, lq2, lk2, subln, lam_init, rope, need_ctx_out):
    scale = DIFF_QK_DIM ** -0.5
    f32 = jnp.float32
    lam = (jnp.exp(jnp.sum(lq1.astype(f32) * lk1.astype(f32)))
           - jnp.exp(jnp.sum(lq2.astype(f32) * lk2.astype(f32))) + lam_init)

    def prep(q, k, v, rotate):
        b, t = q.shape[:2]
        q = q.reshape(b, t, DIFF_HEADS, 2, DIFF_QK_DIM)
        k = k.reshape(b, t, DIFF_HEADS, 2, DIFF_QK_DIM)
        q1, q2, k1, k2 = q[:, :, :, 0], q[:, :, :, 1], k[:, :, :, 0], k[:, :, :, 1]
        if rotate:
            q1, q2, k1, k2 = (apply_rope(a, rope) for a in (q1, q2, k1, k2))
        return q1, q2, k1, k2, v.reshape(b, t, DIFF_HEADS, DIFF_V_DIM)

    def finish(o):
        o = rmsnorm(o, subln) * (1.0 - lam_init)
        return o.reshape(o.shape[0], o.shape[1], -1)

    q1, q2, k1, k2, v = prep(q, k, v, True)
    q1c, q2c, k1c, k2c, vc = prep(qc, kc, vc, False)
    out = diff_attention(q1, q2, jnp.concatenate([k1c, k1], axis=1), jnp.concatenate([k2c, k2], axis=1),
                         jnp.concatenate([vc, v], axis=1), lam, scale)
    out_c = finish(diff_attention(q1c, q2c, k1c, k2c, vc, lam, scale)) if need_ctx_out else None
    return finish(out), out_c


def mixer_window(q, k, v, qc, kc, vc, sinks, rope, need_ctx_out):
    rep = WIN_HEADS // WIN_KV_HEADS
    scale = HEAD_DIM ** -0.5
    sink = sinks.reshape(WIN_KV_HEADS, rep)

    def prep(q, k, v, rotate):
        b, t = q.shape[:2]
        q = q.reshape(b, t, WIN_HEADS, HEAD_DIM)
        k = k.reshape(b, t, WIN_KV_HEADS, HEAD_DIM)
        if rotate:
            q, k = apply_rope(q, rope), apply_rope(k, rope)
        return q.reshape(b, t, WIN_KV_HEADS, rep, HEAD_DIM), k, v.reshape(b, t, WIN_KV_HEADS, HEAD_DIM)

    q, k, v = prep(q, k, v, True)
    qc, kc, vc = prep(qc, kc, vc, False)
    out = windowed_sink_attention(q, k, v, kc, vc, sink, scale)
    out_c = None
    if need_ctx_out:
        oc = sink_attention_block(qc, kc, vc, sink, scale, None)
        out_c = oc.reshape(oc.shape[0], oc.shape[1], -1)
    return out, out_c


def mixer_mla(cq, ckv, kr, cqc, ckvc, krc, q_norm, w_uq, kv_norm, w_ukv, rope, need_ctx_out):
    scale = MLA_QK_DIM ** -0.5

    def prep(cq, ckv, kr, rotate):
        b, t = cq.shape[:2]
        q = (rmsnorm(cq, q_norm) @ w_uq).reshape(b, t, MLA_HEADS, MLA_QK_DIM)
        kv = (rmsnorm(ckv, kv_norm) @ w_ukv).reshape(b, t, MLA_HEADS, MLA_NOPE + MLA_V)
        q_nope, q_rot = q[..., :MLA_NOPE], q[..., MLA_NOPE:]
        k_nope, v = kv[..., :MLA_NOPE], kv[..., MLA_NOPE:]
        if rotate:
            q_rot, kr = apply_rope(q_rot, rope), apply_rope(kr, rope)
        k_rot = jnp.broadcast_to(kr[:, :, None, :], (b, t, MLA_HEADS, MLA_ROPE))
        q = jnp.concatenate([q_nope, q_rot], axis=-1).reshape(b, t, MLA_HEADS, 1, MLA_QK_DIM)
        return q, jnp.concatenate([k_nope, k_rot], axis=-1), v

    q, k, v = prep(cq, ckv, kr, True)
    qc, kc, vc = prep(cqc, ckvc, krc, False)
    out = dense_gqa(q, jnp.concatenate([kc, k], axis=1), jnp.concatenate([vc, v], axis=1), scale)
    out_c = dense_gqa(qc, kc, vc, scale) if need_ctx_out else None
    return out, out_c


def swiglu(h, w_gate, w_up, w_down):
    return (jax.nn.silu(h @ w_gate) * (h @ w_up)) @ w_down


def trunk_layer(x, ctx, mod_lat, mod_ctx, lp, ropes, lam_init, need_ctx_out):
    rope_hd, rope_diff, rope_mla = ropes
    m = jnp.split(mod_lat, 6, axis=-1)
    mc = jnp.split(mod_ctx, 6, axis=-1)
    h = modulate(rmsnorm(x, lp['norm_pre_mix']), m[0], m[1])
    hc = modulate(rmsnorm(ctx, lp['norm_pre_mix']), mc[0], mc[1])
    p = split_columns(h @ lp['w_in'])
    pc = split_columns(hc @ lp['w_in'])
    a, ac = mixer_gqa(p[0], p[1], p[2], pc[0], pc[1], pc[2], lp['gqa_q_norm'], lp['gqa_k_norm'],
                      rope_hd, need_ctx_out)
    b, bc = mixer_diff(p[3], p[4], p[5], pc[3], pc[4], pc[5], lp['diff_lambda_q1'], lp['diff_lambda_k1'],
                       lp['diff_lambda_q2'], lp['diff_lambda_k2'], lp['diff_subln'], lam_init,
                       rope_diff, need_ctx_out)
    w, wc = mixer_window(p[6], p[7], p[8], pc[6], pc[7], pc[8], lp['win_sinks'], rope_hd, need_ctx_out)
    d, dc = mixer_mla(p[9], p[10], p[11], pc[9], pc[10], pc[11], lp['mla_q_norm'], lp['mla_w_uq'],
                      lp['mla_kv_norm'], lp['mla_w_ukv'], rope_mla, need_ctx_out)
    y = jnp.concatenate([a, b, w, d], axis=-1) @ lp['w_out']
    x = x + m[2] * rmsnorm(y, lp['norm_post_mix'])
    h = modulate(rmsnorm(x, lp['norm_pre_ffn']), m[3], m[4])
    x = x + m[5] * rmsnorm(swiglu(h, lp['ffn_w_gate'], lp['ffn_w_up'], lp['ffn_w_down']), lp['norm_post_ffn'])
    if need_ctx_out:
        yc = jnp.concatenate([ac, bc, wc, dc], axis=-1) @ lp['w_out']
        ctx = ctx + mc[2] * rmsnorm(yc, lp['norm_post_mix'])
        hc = modulate(rmsnorm(ctx, lp['norm_pre_ffn']), mc[3], mc[4])
        ctx = ctx + mc[5] * rmsnorm(swiglu(hc, lp['ffn_w_gate'], lp['ffn_w_up'], lp['ffn_w_down']),
                                    lp['norm_post_ffn'])
    return x, ctx


def setup_inputs(seed: int = 0) -> dict:
    key = jax.random.key(seed)
    ks = jax.random.split(key, 27)
    f32 = jnp.float32
    L = DEPTH

    def normal(k, shape, scale):
        return jax.random.normal(k, shape, f32) * scale

    def gain(k, shape):
        return 1.0 + 0.05 * jax.random.normal(k, shape, f32)

    return {
        'x': normal(ks[0], (BATCH, SEQ, D_MODEL), 1.0),
        'c': normal(ks[1], (BATCH, D_MODEL), 1.0),
        'ctx': normal(ks[2], (BATCH, CTX_LEN, D_MODEL), 1.0),
        'c_ctx': normal(ks[3], (D_MODEL,), 1.0),
        'ada_w': normal(ks[4], (L, D_MODEL, 6 * D_MODEL), 0.5 * D_MODEL ** -0.5),
        'ada_b': normal(ks[5], (L, 6 * D_MODEL), 0.01),
        'norm_pre_mix': gain(ks[6], (L, D_MODEL)),
        'norm_post_mix': gain(ks[7], (L, D_MODEL)),
        'norm_pre_ffn': gain(ks[8], (L, D_MODEL)),
        'norm_post_ffn': gain(ks[9], (L, D_MODEL)),
        'w_in': normal(ks[10], (L, D_MODEL, IN_COLS), D_MODEL ** -0.5),
        'gqa_q_norm': gain(ks[11], (L, HEAD_DIM)),
        'gqa_k_norm': gain(ks[12], (L, HEAD_DIM)),
        'diff_lambda_q1': normal(ks[13], (L, DIFF_QK_DIM), 0.1),
        'diff_lambda_k1': normal(ks[14], (L, DIFF_QK_DIM), 0.1),
        'diff_lambda_q2': normal(ks[15], (L, DIFF_QK_DIM), 0.1),
        'diff_lambda_k2': normal(ks[16], (L, DIFF_QK_DIM), 0.1),
        'diff_subln': gain(ks[17], (L, DIFF_V_DIM)),
        'win_sinks': normal(ks[18], (L, WIN_HEADS), 0.5),
        'mla_q_norm': gain(ks[19], (L, MLA_Q_RANK)),
        'mla_w_uq': normal(ks[20], (L, MLA_Q_RANK, MLA_HEADS * MLA_QK_DIM), MLA_Q_RANK ** -0.5),
        'mla_kv_norm': gain(ks[21], (L, MLA_KV_RANK)),
        'mla_w_ukv': normal(ks[22], (L, MLA_KV_RANK, MLA_HEADS * (MLA_NOPE + MLA_V)), MLA_KV_RANK ** -0.5),
        'w_out': normal(ks[23], (L, MIX_WIDTH, D_MODEL), MIX_WIDTH ** -0.5),
        'ffn_w_gate': normal(ks[24], (L, D_MODEL, D_FF), D_MODEL ** -0.5),
        'ffn_w_up': normal(ks[25], (L, D_MODEL, D_FF), D_MODEL ** -0.5),
        'ffn_w_down': normal(ks[26], (L, D_FF, D_MODEL), D_FF ** -0.5),
    }


def reference(x, c, ctx, c_ctx, ada_w, ada_b, norm_pre_mix, norm_post_mix, norm_pre_ffn, norm_post_ffn,
              w_in, gqa_q_norm, gqa_k_norm, diff_lambda_q1, diff_lambda_k1, diff_lambda_q2, diff_lambda_k2,
              diff_subln, win_sinks, mla_q_norm, mla_w_uq, mla_kv_norm, mla_w_ukv, w_out,
              ffn_w_gate, ffn_w_up, ffn_w_down):
    n_tokens = x.shape[1]
    ropes = (axial_rope_tables(n_tokens, HEAD_DIM), axial_rope_tables(n_tokens, DIFF_QK_DIM),
             axial_rope_tables(n_tokens, MLA_ROPE))
    silu_c = jax.nn.silu(c)
    silu_cc = jax.nn.silu(c_ctx)
    for l in range(DEPTH):
        mod_lat = (silu_c @ ada_w[l] + ada_b[l])[:, None, :]
        mod_ctx = (silu_cc @ ada_w[l] + ada_b[l])[None, None, :]
        lp = {
            'norm_pre_mix': norm_pre_mix[l], 'norm_post_mix': norm_post_mix[l],
            'norm_pre_ffn': norm_pre_ffn[l], 'norm_post_ffn': norm_post_ffn[l],
            'w_in': w_in[l], 'gqa_q_norm': gqa_q_norm[l], 'gqa_k_norm': gqa_k_norm[l],
            'diff_lambda_q1': diff_lambda_q1[l], 'diff_lambda_k1': diff_lambda_k1[l],
            'diff_lambda_q2': diff_lambda_q2[l], 'diff_lambda_k2': diff_lambda_k2[l],
            'diff_subln': diff_subln[l], 'win_sinks': win_sinks[l],
            'mla_q_norm': mla_q_norm[l], 'mla_w_uq': mla_w_uq[l],
            'mla_kv_norm': mla_kv_norm[l], 'mla_w_ukv': mla_w_ukv[l],
            'w_out': w_out[l], 'ffn_w_gate': ffn_w_gate[l], 'ffn_w_up': ffn_w_up[l],
            'ffn_w_down': ffn_w_down[l],
        }
        lam_init = 0.8 - 0.6 * math.exp(-0.3 * l)
        x, ctx = trunk_layer(x, ctx, mod_lat, mod_ctx, lp, ropes, lam_init, l < DEPTH - 1)
    return x
```

```python
import math
import os
from contextlib import ExitStack

import numpy as np
import concourse.bass as bass
import concourse.mybir as mybir
from concourse.bass_utils import run_bass_kernel_spmd

F32, BF16 = mybir.dt.float32, mybir.dt.bfloat16
ALU, AF, AX = mybir.AluOpType, mybir.ActivationFunctionType, mybir.AxisListType

D = 2048
CTX = 256
DFF = 5632
INC = 4416
EPS = 1e-6
NP_DMA = 8

WNAMES = ["w_in", "mla_w_uq", "mla_w_ukv", "w_out", "ffn_w_gate", "ffn_w_up", "ffn_w_down"]
WSHAPES = {"w_in": (D, INC), "mla_w_uq": (512, 768), "mla_w_ukv": (256, 1024), "w_out": (D, D),
           "ffn_w_gate": (D, DFF), "ffn_w_up": (D, DFF), "ffn_w_down": (DFF, D)}
SMALL = {"ada_b": (2, 12288), "norm_pre_mix": (2, D), "norm_post_mix": (2, D), "norm_pre_ffn": (2, D),
         "norm_post_ffn": (2, D), "gqa_q_norm": (2, 128), "gqa_k_norm": (2, 128),
         "diff_lambda_q1": (2, 64), "diff_lambda_k1": (2, 64), "diff_lambda_q2": (2, 64),
         "diff_lambda_k2": (2, 64), "diff_subln": (2, 128), "win_sinks": (2, 4),
         "mla_q_norm": (2, 512), "mla_kv_norm": (2, 256)}


class Buf:
    __slots__ = ("w", "r")

    def __init__(self):
        self.w = []
        self.r = []


class T:
    __slots__ = ("t", "b")

    def __init__(self, t):
        self.t = t
        self.b = Buf()


class Ring:
    def __init__(self, items):
        self.items = items
        self.i = 0

    def next(self):
        it = self.items[self.i]
        self.i = (self.i + 1) % len(self.items)
        return it


class Sched:
    def __init__(self, nc, es):
        self.nc = nc
        self.eng = {"pe": nc.tensor, "act": nc.scalar, "dve": nc.vector, "pool": nc.gpsimd, "sp": nc.sync}
        self.sem = {k: es.enter_context(nc.semaphore("s_" + k)) for k in ("pe", "act", "dve", "pool")}
        self.cnt = {k: 0 for k in self.sem}
        self.waited = {k: {} for k in self.eng}
        self.dsem = {q: [es.enter_context(nc.semaphore(f"d_{q}{i}")) for i in range(NP_DMA)] for q in ("sp", "pool")}
        self.dval = {q: [0] * NP_DMA for q in self.dsem}
        self.dnext = {q: 0 for q in self.dsem}
        self.semkey = {}

    def _key(self, sem):
        return id(sem)

    def _wait(self, e, ev):
        sem, val = ev
        if e == "pe" and sem is self.sem["pe"]:
            return
        k = self._key(sem)
        if self.waited[e].get(k, 0) >= val:
            return
        self.eng[e].wait_ge(sem, val)
        self.waited[e][k] = val

    def _deps(self, e, reads, writes):
        evs = {}
        def add(ev):
            k = self._key(ev[0])
            if k not in evs or evs[k][1] < ev[1]:
                evs[k] = ev
        for b in reads:
            for ev in b.b.w:
                add(ev)
        for b in writes:
            for ev in b.b.w:
                add(ev)
            for ev in b.b.r:
                add(ev)
        for ev in evs.values():
            self._wait(e, ev)

    def op(self, e, fn, reads=(), writes=(), sig=True):
        self._deps(e, reads, writes)
        ins = fn(self.eng[e])
        if sig:
            self.cnt[e] += 1
            ins.then_inc(self.sem[e], 1)
            ev = (self.sem[e], self.cnt[e])
        else:
            ev = (self.sem[e], self.cnt[e] + 1)
        for b in reads:
            b.b.r.append(ev)
        for b in writes:
            b.b.w = [ev]
            b.b.r = []
        return ev

    def dma(self, q, out, in_, reads=(), writes=(), acc=False, slow=False):
        self._deps(q, reads, writes)
        i = self.dnext[q]
        self.dnext[q] = (i + 1) % NP_DMA
        sem = self.dsem[q][i]
        if self.dval[q][i] > 0:
            self._wait(q, (sem, self.dval[q][i]))
        self.dval[q][i] += 16
        if slow:
            self.eng[q].dma_start(out=out, in_=in_, allow_slow_non_contiguous=True).then_inc(sem, 16)
        else:
            self.eng[q].dma_start(out=out, in_=in_).then_inc(sem, 16)
        ev = (sem, self.dval[q][i])
        for b in reads:
            b.b.r.append(ev)
        for b in writes:
            if acc:
                b.b.w.append(ev)
            else:
                b.b.w = [ev]
                b.b.r = []
        return ev

    def coll(self, ins_ap, outs_ap, groups, reads=(), writes=()):
        q = "pool"
        self._deps(q, reads, writes)
        i = self.dnext[q]
        self.dnext[q] = (i + 1) % NP_DMA
        sem = self.dsem[q][i]
        if self.dval[q][i] > 0:
            self._wait(q, (sem, self.dval[q][i]))
        self.dval[q][i] += 16
        self.nc.gpsimd.collective_compute("AllGather", ALU.bypass, replica_groups=groups,
                                          ins=[ins_ap], outs=[outs_ap]).then_inc(sem, 16)
        ev = (sem, self.dval[q][i])
        for b in reads:
            b.b.r.append(ev)
        for b in writes:
            b.b.w = [ev]
            b.b.r = []
        return ev

    def barrier(self):
        evs = [(self.sem[k], self.cnt[k]) for k in self.sem if self.cnt[k] > 0]
        for q in self.dsem:
            evs += [(s, v) for s, v in zip(self.dsem[q], self.dval[q]) if v > 0]
        for e in self.eng:
            for ev in evs:
                self._wait(e, ev)


def host_consts(Tlat):
    rows = Tlat // 64
    row = np.repeat(np.arange(rows), 64).astype(np.float32)
    col = np.tile(np.arange(64), rows).astype(np.float32)
    out = {}
    for dim, nm in ((128, "128"), (64, "64")):
        q = dim // 4
        inv = (np.float32(10000.0) ** (-np.arange(q, dtype=np.float32) / np.float32(q))).astype(np.float32)
        ang = np.concatenate([row[:, None] * inv, col[:, None] * inv], axis=-1).astype(np.float32)
        cos, sin = np.cos(ang).astype(np.float32), np.sin(ang).astype(np.float32)
        half = dim // 2
        idx = np.arange(128) % half
        out["rc" + nm] = np.ascontiguousarray(cos[:, idx].T)
        out["rs" + nm] = np.ascontiguousarray(sin[:, idx].T)
    rt128 = np.zeros((128, 128), np.float32)
    for m in range(64):
        rt128[m + 64, m] = -1.0
    for m in range(64, 128):
        rt128[m - 64, m] = 1.0
    rt64 = np.zeros((128, 128), np.float32)
    for g0 in (0, 64):
        for i in range(32):
            rt64[g0 + i + 32, g0 + i] = -1.0
        for i in range(32, 64):
            rt64[g0 + i - 32, g0 + i] = 1.0
    p = np.arange(128)[:, None]
    n = np.arange(512)[None, :]
    masks = np.zeros((128, 6, 512), np.float32)
    for dl in range(6):
        masks[:, dl, :] = (np.abs(128 * (dl - 1) + p - n) <= 128).astype(np.float32)
    hm = np.zeros((128, 4), np.float32)
    hm[:64, 0] = 1.0
    hm[64:, 1] = 1.0
    cm = np.concatenate([np.eye(128, dtype=np.float32), np.ones((128, 128), np.float32), rt128, rt64, hm], axis=1)
    out["cmats"] = cm
    out["wmask"] = masks.reshape(128, 6 * 512)
    return out


class Builder:
    def __init__(self, NT, nlayers=2, debug=False, G=1):
        self.G = G
        self.NT = NT
        self.Tl = NT * 512
        self.NTOK = self.Tl + CTX
        self.NCH = self.NTOK // 128
        self.nlayers = nlayers
        self.debug = debug

    def uname(self, name):
        self._uid = getattr(self, "_uid", 0) + 1
        return f"{name}_{self._uid}"

    def build(self):
        nc = bass.Bass("TRN2", target_bir_lowering=False)
        self.nc = nc
        Tl, NTOK = self.Tl, self.NTOK
        di = lambda name, shape, dt=F32: nc.dram_tensor(name, list(shape), dt, kind="ExternalInput").ap()
        ds = lambda name, shape, dt=BF16: nc.dram_tensor(name, list(shape), dt, kind="Internal").ap()
        self.x_in = di("x", (Tl, D))
        self.ctx_in = di("ctx", (CTX, D))
        self.cc_in = di("cc", (2, D))
        G = self.G
        self.ada_w = di("ada_w", (2, D // G, 6 * D))
        self.w = {n: di(n, (2, WSHAPES[n][0] // G, WSHAPES[n][1])) for n in WNAMES}
        if G > 1:
            self.sh = [{n: T(ds(f"sh_{n}{l}", (WSHAPES[n][0] // G, WSHAPES[n][1]))) for n in WNAMES} for l in range(2)]
            self.ash = [T(ds(f"ash{l}", (D // G, 6 * D), F32)) for l in range(2)]
            self.adaf = [T(ds(f"adaf{l}", (D, 6 * D), F32)) for l in range(2)]
        self.sm = {n: di(n, s) for n, s in SMALL.items()}
        self.c_rc128 = di("rc128", (128, Tl))
        self.c_rs128 = di("rs128", (128, Tl))
        self.c_rc64 = di("rc64", (128, Tl))
        self.c_rs64 = di("rs64", (128, Tl))
        self.c_mats = di("cmats", (128, 516))
        self.c_mask = di("wmask", (128, 6 * 512))
        self.out = nc.dram_tensor("out", [Tl, D], F32, kind="ExternalOutput").ap()
        self.wb = [{n: T(ds(f"wb_{n}{l}", WSHAPES[n])) for n in WNAMES} for l in range(2)]
        self.derd = [T(ds(f"derd{l}", (2, 6, D), F32)) for l in range(2)]
        self.abd = [T(ds(f"abd{l}", (128, 160), F32)) for l in range(2)]
        self.qT = [T(ds(f"qT{l}", (24, 128, NTOK))) for l in range(2)]
        self.kT = [T(ds(f"kT{l}", (13, 128, NTOK))) for l in range(2)]
        self.vv = [T(ds(f"vv{l}", (NTOK, 1536))) for l in range(2)]
        self.aT = [T(ds(f"aT{l}", (D, NTOK))) for l in range(2)]
        self.x1 = T(ds("x1", (NTOK, D), F32))
        with ExitStack() as es:
            self.S = S = Sched(nc, es)
            al = lambda name, shape, dt: T(es.enter_context(nc.sbuf_tensor(self.uname(name), list(shape), dt)))
            self.cmb = al("cmb", (128, 512), BF16)
            self.cmf = al("cmf", (128, 128), F32)
            S.dma("pool", self.cmb.t[:], self.c_mats[:, 0:512], writes=[self.cmb])
            S.dma("sp", self.cmf.t[:], self.c_mats[:, 0:128], writes=[self.cmf])
            self.onesf = al("onesf", (1, 128), F32)
            S.dma("sp", self.onesf.t[:], self.c_mats[0:1, 128:256], writes=[self.onesf])
            self.hmask = al("hmask", (128, 4), F32)
            S.dma("sp", self.hmask.t[:], self.c_mats[:, 512:516], writes=[self.hmask])
            self.ident = self.cmb.t[:, 0:128]
            self.ones = self.cmb.t[:, 128:256]
            self.rt = {128: self.cmb.t[:, 256:384], 64: self.cmb.t[:, 384:512]}
            stop = int(os.environ.get("K_STOP", "99"))
            S.barrier()
            if stop >= 1:
                self.prologue()
                S.barrier()
            step = 1
            for l in range(self.nlayers):
                for ph in (self.phaseA, self.phaseB, self.phaseC):
                    step += 1
                    if stop >= step:
                        ph(l)
                        S.barrier()
        return nc

    def prologue(self):
        nc, S = self.nc, self.S
        G = self.G
        groups = [list(range(G))]
        if G > 1:
            for l in range(self.nlayers):
                S.dma("sp", self.ash[l].t[:], self.ada_w[l, :, :], writes=[self.ash[l]])
                S.coll(self.ash[l].t[:], self.adaf[l].t[:], groups, reads=[self.ash[l]], writes=[self.adaf[l]])
        for l in range(self.nlayers):
            for n in WNAMES:
                rows, cols = WSHAPES[n]
                if G > 1:
                    sh = self.sh[l][n]
                    S.dma("pool", sh.t[:], self.w[n][l, :, :], writes=[sh])
                    S.coll(sh.t[:], self.wb[l][n].t[:], groups, reads=[sh], writes=[self.wb[l][n]])
                    continue
                step = 512 if cols > 2048 else 1024
                for r0 in range(0, rows, step):
                    r1 = min(rows, r0 + step)
                    S.dma("pool", self.wb[l][n].t[r0:r1, :], self.w[n][l, r0:r1, :], writes=[self.wb[l][n]], acc=True)
        with ExitStack() as es:
            al = lambda name, shape, dt: T(es.enter_context(nc.sbuf_tensor(self.uname(name), list(shape), dt)))
            pal = lambda name, shape, dt: T(es.enter_context(nc.psum_tensor(self.uname(name), list(shape), dt)))
            craw = al("craw", (2, D), F32)
            csil = al("csil", (2, D), F32)
            sc = al("sc", (128, 16, 2), F32)
            S.dma("sp", craw.t[:], self.cc_in, writes=[craw])
            S.op("act", lambda e: e.activation(out=csil.t[:], in_=craw.t[:], func=AF.Silu), reads=[craw], writes=[csil])
            ptp = pal("ptp", (128, 16, 2), F32)
            for kc in range(16):
                S.op("pe", lambda e: e.matmul(ptp.t[:, kc, :], lhsT=csil.t[0:2, kc * 128:(kc + 1) * 128],
                                              rhs=self.cmf.t[0:2, 0:2], start=True, stop=True),
                     reads=[csil, self.cmf], writes=[ptp], sig=(kc == 15))
            S.op("dve", lambda e: e.tensor_copy(out=sc.t[:], in_=ptp.t[:]), reads=[ptp], writes=[sc])
            waR = Ring([al(f"wa{i}", (128, 16, 256), F32) for i in range(2)])
            biR = Ring([al(f"bi{i}", (2, 256), F32) for i in range(2)])
            pmR = Ring([pal(f"pm{i}", (128, 512), F32) for i in range(2)])
            mrow1 = al("mrow", (2, 6 * D), F32)
            mrow = [mrow1, mrow1]
            grow = al("grow", (2, 4, D), F32)
            der = al("der", (2, 6, D), F32)
            prow = al("prow", (1, 1152), F32)
            abs_ = al("abs", (128, 160), F32)
            pab = pal("pab", (128, 160), F32)
            for l in range(self.nlayers):
                for nch in range(48):
                    wa, bi, pm = waR.next(), biR.next(), pmR.next()
                    if G > 1:
                        S.dma("sp", wa.t[:], self.adaf[l].t[:, nch * 256:(nch + 1) * 256].rearrange("(kc p) n -> p kc n", p=128),
                              reads=[self.adaf[l]], writes=[wa])
                    else:
                        S.dma("sp", wa.t[:], self.ada_w[l, :, nch * 256:(nch + 1) * 256].rearrange("(kc p) n -> p kc n", p=128),
                              writes=[wa])
                    for r in range(2):
                        S.dma("sp", bi.t[r:r + 1, :], self.sm["ada_b"][l:l + 1, nch * 256:(nch + 1) * 256],
                              writes=[bi], acc=(r == 1))
                    for kc in range(16):
                        S.op("pe", lambda e: e.matmul(pm.t[0:2, 0:256], lhsT=sc.t[:, kc, :], rhs=wa.t[:, kc, :],
                                                      start=(kc == 0), stop=(kc == 15)),
                             reads=[sc, wa], writes=[pm], sig=(kc == 15))
                    S.op("dve", lambda e: e.tensor_tensor(out=mrow[l].t[:, nch * 256:(nch + 1) * 256], in0=pm.t[0:2, 0:256],
                                                          in1=bi.t[:], op=ALU.add), reads=[pm, bi], writes=[mrow[l]])
                for j, nm in enumerate(["norm_pre_mix", "norm_post_mix", "norm_pre_ffn", "norm_post_ffn"]):
                    for r in range(2):
                        S.dma("sp", grow.t[r:r + 1, j, :], self.sm[nm][l:l + 1, :], writes=[grow], acc=not (j == 0 and r == 0))
                m = lambda i: mrow[l].t[:, i * D:(i + 1) * D]
                rd = [mrow[l], grow]
                S.op("dve", lambda e: e.scalar_tensor_tensor(out=der.t[:, 0, :], in0=m(1), scalar=1.0, in1=grow.t[:, 0, :],
                                                             op0=ALU.add, op1=ALU.mult), reads=rd, writes=[der])
                S.op("dve", lambda e: e.tensor_copy(out=der.t[:, 1, :], in_=m(0)), reads=rd, writes=[der])
                S.op("dve", lambda e: e.tensor_tensor(out=der.t[:, 2, :], in0=m(2), in1=grow.t[:, 1, :], op=ALU.mult),
                     reads=rd, writes=[der])
                S.op("dve", lambda e: e.scalar_tensor_tensor(out=der.t[:, 3, :], in0=m(4), scalar=1.0, in1=grow.t[:, 2, :],
                                                             op0=ALU.add, op1=ALU.mult), reads=rd, writes=[der])
                S.op("dve", lambda e: e.tensor_copy(out=der.t[:, 4, :], in_=m(3)), reads=rd, writes=[der])
                S.op("dve", lambda e: e.tensor_tensor(out=der.t[:, 5, :], in0=m(5), in1=grow.t[:, 3, :], op=ALU.mult),
                     reads=rd, writes=[der])
                S.dma("pool", self.derd[l].t[:], der.t[:], reads=[der], writes=[self.derd[l]])
                off = 0
                for nm, ln in (("mla_q_norm", 512), ("mla_kv_norm", 256), ("gqa_q_norm", 128), ("gqa_k_norm", 128), ("diff_subln", 128)):
                    S.dma("sp", prow.t[0:1, off:off + ln], self.sm[nm][l:l + 1, :], writes=[prow], acc=(off > 0))
                    off += ln
                for ph, js in enumerate(((0, 1), (3, 4))):
                    for jj, j in enumerate(js):
                        for kc in range(16):
                            c0 = ph * 64 + (jj * 16 + kc) * 2
                            S.op("pe", lambda e: e.matmul(pab.t[:, c0:c0 + 2], lhsT=der.t[0:2, j, kc * 128:(kc + 1) * 128],
                                                          rhs=self.cmf.t[0:2, 0:2], start=True, stop=True),
                                 reads=[der, self.cmf], writes=[pab], sig=False)
                for i in range(9):
                    S.op("pe", lambda e: e.matmul(pab.t[:, 128 + i:129 + i], lhsT=prow.t[0:1, i * 128:(i + 1) * 128],
                                                  rhs=self.cmf.t[0:1, 0:1], start=True, stop=True),
                         reads=[prow, self.cmf], writes=[pab], sig=(i == 8))
                S.op("dve", lambda e: e.memset(abs_.t[:], 0.0), writes=[abs_])
                S.op("dve", lambda e: e.tensor_copy(out=abs_.t[:, 0:137], in_=pab.t[:, 0:137]), reads=[pab], writes=[abs_])
                S.dma("pool", self.abd[l].t[:], abs_.t[:], reads=[abs_], writes=[self.abd[l]])
            S.barrier()

    def tiles(self, l, with_ctx=True):
        ts = [(j * 512, 512, False) for j in range(self.NT)]
        if with_ctx:
            ts.append((self.Tl, CTX, True))
        return ts

    def xsrc(self, l, tok0, is_ctx):
        if l == 0:
            if is_ctx:
                return self.ctx_in, tok0 - self.Tl, None
            return self.x_in, tok0, None
        return self.x1.t, tok0, self.x1

    def load_pp(self, al, name, src_row_ap, nk):
        t = al(name, (128, nk), F32)
        self.S.dma("sp", t.t[:], src_row_ap.rearrange("(kc p) -> p kc", p=128), writes=[t], slow=True)
        return t

    def norm_stats(self, src_ap, src_T, junk, st, width):
        S = self.S
        S.op("dve", lambda e: e.memset(st.t[:], 0.0), writes=[st])
        S.op("act", lambda e: e.activation(out=junk.t[:, 0:width], in_=src_ap, func=AF.Square, accum_out=st.t[:, 0:1]),
             reads=[src_T, st], writes=[junk, st])
        S.op("act", lambda e: e.activation(out=st.t[:, 1:2], in_=st.t[:, 0:1], func=AF.Sqrt, scale=1.0 / width, bias=EPS),
             reads=[st], writes=[st])
        S.op("dve", lambda e: e.reciprocal(out=st.t[:, 2:3], in_=st.t[:, 1:2]), reads=[st], writes=[st])

    def to_hT(self, xsrc_T, st, xn, ptrR, hT, s, AB, row):
        S = self.S
        S.op("act", lambda e: e.activation(out=xn.t[:], in_=xsrc_T.t[:], func=AF.Copy, scale=st.t[:, 2:3]),
             reads=[xsrc_T, st], writes=[xn])
        for half in range(2):
            ptr = ptrR.next()
            for j in range(8):
                kc = half * 8 + j
                S.op("pe", lambda e: e.transpose(out=ptr.t[:, j * 128:(j + 1) * 128], in_=xn.t[:, kc * 128:(kc + 1) * 128],
                                                 identity=self.ident), reads=[xn, self.cmb], writes=[ptr], sig=(j == 7))
            for j in range(8):
                kc = half * 8 + j
                S.op("dve", lambda e: e.tensor_scalar(out=hT.t[:, kc, s * 128:(s + 1) * 128], in0=ptr.t[:, j * 128:(j + 1) * 128],
                                                      scalar1=AB.t[:, 0, kc, row:row + 1], scalar2=AB.t[:, 1, kc, row:row + 1],
                                                      op0=ALU.mult, op1=ALU.add), reads=[ptr, AB], writes=[hT])

    def phaseA(self, l):
        nc, S = self.nc, self.S
        wb = self.wb[l]
        qT, kT, vv = self.qT[l], self.kT[l], self.vv[l]
        with ExitStack() as es:
            al = lambda name, shape, dt: T(es.enter_context(nc.sbuf_tensor(self.uname(name), list(shape), dt)))
            pal = lambda name, shape, dt: T(es.enter_context(nc.psum_tensor(self.uname(name), list(shape), dt)))
            AB = al("AB", (128, 2, 16, 2), F32)
            S.dma("sp", AB.t[:], self.abd[l].t[:, 0:64].rearrange("p (j k r) -> p j k r", j=2, r=2), reads=[self.abd[l]], writes=[AB])
            prm = al("prm", (128, 16), F32)
            S.dma("sp", prm.t[:, 0:9], self.abd[l].t[:, 128:137], reads=[self.abd[l]], writes=[prm])
            mqn = T(prm.t[:, 0:4]); mqn.b = prm.b
            mkn = T(prm.t[:, 4:6]); mkn.b = prm.b
            gq = T(prm.t[:, 6:7]); gq.b = prm.b
            gk = T(prm.t[:, 7:8]); gk.b = prm.b
            wuq = al("wuq", (128, 4, 768), BF16)
            wukv = al("wukv", (128, 2, 1024), BF16)
            S.dma("sp", wuq.t[:], wb["mla_w_uq"].t.rearrange("(kc p) n -> p kc n", p=128), reads=[wb["mla_w_uq"]], writes=[wuq])
            S.dma("sp", wukv.t[:], wb["mla_w_ukv"].t.rearrange("(kc p) n -> p kc n", p=128), reads=[wb["mla_w_ukv"]], writes=[wukv])
            xsR = Ring([al(f"xs{i}", (128, D), F32) for i in range(2)])
            xnR = Ring([al(f"xn{i}", (128, D), BF16) for i in range(2)])
            junk = al("junk", (128, D), BF16)
            hTR = Ring([al(f"hT{i}", (128, 16, 512), BF16) for i in range(2)])
            wchR = Ring([al(f"wch{i}", (128, 16, 512), BF16) for i in range(3)])
            stR = Ring([al(f"st{i}", (128, 4), F32) for i in range(4)])
            mk = lambda nm, k, dt: Ring([al(f"{nm}{i}", (128, 512), dt) for i in range(k)])
            xbR, sqR, stdR, rstdR = mk("xb", 2, BF16), mk("sq", 2, BF16), mk("std", 2, F32), mk("rstd", 2, F32)
            t1R, t2R, obR, vstR = mk("t1", 2, F32), mk("t2", 2, F32), mk("ob", 4, BF16), mk("vst", 2, BF16)
            cq = al("cq", (128, 4, 512), F32)
            cqn = al("cqn", (128, 4, 512), BF16)
            tabR = {k: Ring([al(f"tab{k}{i}", (128, 512), F32) for i in range(2)]) for k in ("c128", "s128", "c64", "s64")}
            tabsrc = {"c128": self.c_rc128, "s128": self.c_rs128, "c64": self.c_rc64, "s64": self.c_rs64}
            ptrR = Ring([pal(f"ptr{i}", (128, 1024), BF16) for i in range(2)])
            paccR = Ring([pal(f"pacc{i}", (128, 512), F32) for i in range(3)])
            pauxR = Ring([pal(f"paux{i}", (128, 512), F32) for i in range(3)])

            for (tok0, n, is_ctx) in self.tiles(l):
                row = 1 if is_ctx else 0
                tabs = None
                if not is_ctx:
                    tabs = {}
                    for k in tabR:
                        tt = tabR[k].next()
                        S.dma("sp", tt.t[:, 0:n], tabsrc[k][:, tok0:tok0 + n], writes=[tt])
                        tabs[k] = tt
                hT = hTR.next()
                src, r0, srcT = self.xsrc(l, tok0, is_ctx)
                for s in range(n // 128):
                    if int(os.environ.get("K_ASTOP", "99")) <= 1:
                        break
                    xs, st, xn = xsR.next(), stR.next(), xnR.next()
                    S.dma("sp", xs.t[:], src[r0 + s * 128:r0 + (s + 1) * 128, :], reads=[srcT] if srcT else [], writes=[xs])
                    self.norm_stats(xs.t[:], xs, junk, st, D)
                    self.to_hT(xs, st, xn, ptrR, hT, s, AB, row)

                astop = int(os.environ.get("K_ASTOP", "99"))
                if astop <= 2:
                    continue

                def store(dst_T, dst_ap, src_T, src_ap):
                    S.dma("pool", dst_ap, src_ap, reads=[src_T], writes=[dst_T], acc=True)

                def rope_store(xb, M, kind, dst_T, dst_ap, halves=None):
                    fin = xb
                    if not is_ctx:
                        C, Sn = tabs[f"c{kind}"], tabs[f"s{kind}"]
                        paux, t1, t2, ob = pauxR.next(), t1R.next(), t2R.next(), obR.next()
                        S.op("pe", lambda e: e.matmul(paux.t[0:M, 0:n], lhsT=self.rt[kind][0:M, 0:M], rhs=xb.t[0:M, 0:n],
                                                      start=True, stop=True), reads=[xb, self.cmb], writes=[paux])
                        S.op("dve", lambda e: e.tensor_tensor(out=t1.t[0:M, 0:n], in0=xb.t[0:M, 0:n], in1=C.t[0:M, 0:n], op=ALU.mult),
                             reads=[xb, C], writes=[t1])
                        S.op("dve", lambda e: e.tensor_tensor(out=t2.t[0:M, 0:n], in0=paux.t[0:M, 0:n], in1=Sn.t[0:M, 0:n], op=ALU.mult),
                             reads=[paux, Sn], writes=[t2])
                        S.op("dve", lambda e: e.tensor_tensor(out=ob.t[0:M, 0:n], in0=t1.t[0:M, 0:n], in1=t2.t[0:M, 0:n], op=ALU.add),
                             reads=[t1, t2], writes=[ob])
                        fin = ob
                    if halves is None:
                        store(dst_T, dst_ap, fin, fin.t[0:M, 0:n])
                        return
                    for (mc, dap) in halves:
                        om = obR.next()
                        S.op("dve", lambda e: e.tensor_scalar(out=om.t[:, 0:n], in0=fin.t[:, 0:n], scalar1=self.hmask.t[:, mc:mc + 1],
                                                              scalar2=self.hmask.t[:, 2:3], op0=ALU.mult, op1=ALU.add),
                             reads=[fin, self.hmask], writes=[om])
                        store(dst_T, dap, om, om.t[:, 0:n])

                def proj_fm(wt, wap_fn, nk, rhs_T, rhs_fn, M):
                    pacc = paccR.next()
                    for kc in range(nk):
                        S.op("pe", lambda e: e.matmul(pacc.t[0:M, 0:n], lhsT=wap_fn(kc), rhs=rhs_fn(kc),
                                                      start=(kc == 0), stop=(kc == nk - 1)),
                             reads=[wt, rhs_T], writes=[pacc], sig=(kc == nk - 1))
                    return pacc

                def plain_block(wch, col0, kind, dst_T, dst_ap, M=128, halves=None):
                    pacc = proj_fm(wch, lambda kc: wch.t[:, kc, col0:col0 + M], 16, hT, lambda kc: hT.t[:, kc, 0:n], M)
                    xb = xbR.next()
                    S.op("act", lambda e: e.activation(out=xb.t[0:M, 0:n], in_=pacc.t[0:M, 0:n], func=AF.Copy),
                         reads=[pacc], writes=[xb])
                    rope_store(xb, M, kind, dst_T, dst_ap, halves)

                def rstd_from_sumsq(paux, width):
                    std, rstd = stdR.next(), rstdR.next()
                    S.op("act", lambda e: e.activation(out=std.t[:, 0:n], in_=paux.t[:, 0:n], func=AF.Sqrt,
                                                       scale=1.0 / width, bias=EPS), reads=[paux], writes=[std])
                    S.op("dve", lambda e: e.reciprocal(out=rstd.t[:, 0:n], in_=std.t[:, 0:n]), reads=[std], writes=[rstd])
                    return rstd

                def gqa_block(wch, col0, gain, dst_T, dst_ap):
                    pacc = proj_fm(wch, lambda kc: wch.t[:, kc, col0:col0 + 128], 16, hT, lambda kc: hT.t[:, kc, 0:n], 128)
                    sq, paux = sqR.next(), pauxR.next()
                    S.op("act", lambda e: e.activation(out=sq.t[:, 0:n], in_=pacc.t[:, 0:n], func=AF.Square),
                         reads=[pacc], writes=[sq])
                    S.op("pe", lambda e: e.matmul(paux.t[:, 0:n], lhsT=self.ones, rhs=sq.t[:, 0:n], start=True, stop=True),
                         reads=[sq, self.cmb], writes=[paux])
                    rstd = rstd_from_sumsq(paux, 128)
                    xb = xbR.next()
                    S.op("dve", lambda e: e.scalar_tensor_tensor(out=xb.t[:, 0:n], in0=pacc.t[:, 0:n], scalar=gain.t[:, 0:1],
                                                                 in1=rstd.t[:, 0:n], op0=ALU.mult, op1=ALU.mult),
                         reads=[pacc, gain, rstd], writes=[xb])
                    rope_store(xb, 128, 128, dst_T, dst_ap)

                def proj_tm(wt, rhs_fn, nk, lhs_T, lhs_fn, ncols, vcol0):
                    for s in range(n // 128):
                        pacc, vst = paccR.next(), vstR.next()
                        for kc in range(nk):
                            S.op("pe", lambda e: e.matmul(rhs_fn(pacc, kc)[0], lhsT=lhs_fn(kc, s), rhs=rhs_fn(pacc, kc)[1],
                                                          start=(kc == 0), stop=(kc == nk - 1)),
                                 reads=[wt, lhs_T], writes=[pacc], sig=(kc == nk - 1))
                        S.op("act", lambda e: e.activation(out=vst.t[:, 0:ncols], in_=pacc.t[:, 0:ncols], func=AF.Copy),
                             reads=[pacc], writes=[vst])
                        store(vv, vv.t[tok0 + s * 128:tok0 + (s + 1) * 128, vcol0:vcol0 + ncols], vst, vst.t[:, 0:ncols])

                def v_block(wch, col0, ncols, vcol0):
                    proj_tm(wch, lambda pacc, kc: (pacc.t[:, 0:ncols], wch.t[:, kc, col0:col0 + ncols]), 16,
                            hT, lambda kc, s: hT.t[:, kc, s * 128:(s + 1) * 128], ncols, vcol0)

                def norm_fm(wch, nblk, width, gains, raw, outn):
                    ssacc = t1R.next()
                    for j in range(nblk):
                        pacc = proj_fm(wch, lambda kc: wch.t[:, kc, j * 128:(j + 1) * 128], 16, hT, lambda kc: hT.t[:, kc, 0:n], 128)
                        sq, paux = sqR.next(), pauxR.next()
                        S.op("dve", lambda e: e.tensor_copy(out=raw.t[:, j, 0:n], in_=pacc.t[:, 0:n]), reads=[pacc], writes=[raw])
                        S.op("act", lambda e: e.activation(out=sq.t[:, 0:n], in_=raw.t[:, j, 0:n], func=AF.Square),
                             reads=[raw], writes=[sq])
                        S.op("pe", lambda e: e.matmul(paux.t[:, 0:n], lhsT=self.ones, rhs=sq.t[:, 0:n], start=True, stop=True),
                             reads=[sq, self.cmb], writes=[paux])
                        if j == 0:
                            S.op("dve", lambda e: e.tensor_copy(out=ssacc.t[:, 0:n], in_=paux.t[:, 0:n]), reads=[paux], writes=[ssacc])
                        else:
                            S.op("dve", lambda e: e.tensor_tensor(out=ssacc.t[:, 0:n], in0=ssacc.t[:, 0:n], in1=paux.t[:, 0:n], op=ALU.add),
                                 reads=[ssacc, paux], writes=[ssacc])
                    rstd = rstd_from_sumsq(ssacc, width)
                    for j in range(nblk):
                        S.op("dve", lambda e: e.scalar_tensor_tensor(out=outn.t[:, j, 0:n], in0=raw.t[:, j, 0:n],
                                                                     scalar=gains.t[:, j:j + 1], in1=rstd.t[:, 0:n],
                                                                     op0=ALU.mult, op1=ALU.mult),
                             reads=[raw, gains, rstd], writes=[outn])

                def up_plain(wt, col0, nk, src, dst_T, dst_ap):
                    pacc = proj_fm(wt, lambda kc: wt.t[:, kc, col0:col0 + 128], nk, src, lambda kc: src.t[:, kc, 0:n], 128)
                    ob = obR.next()
                    S.op("act", lambda e: e.activation(out=ob.t[:, 0:n], in_=pacc.t[:, 0:n], func=AF.Copy), reads=[pacc], writes=[ob])
                    store(dst_T, dst_ap, ob, ob.t[:, 0:n])

                tsl = slice(tok0, tok0 + n)
                for c in range(9):
                    if c > astop - 3:
                        break
                    wch = wchR.next()
                    wcols = 512 if c < 8 else 320
                    S.dma("sp", wch.t[:, :, 0:wcols], wb["w_in"].t[:, c * 512:c * 512 + wcols].rearrange("(kc p) n -> p kc n", p=128),
                          reads=[wb["w_in"]], writes=[wch])
                    if c == 0:
                        for h in range(4):
                            gqa_block(wch, h * 128, gq, qT, qT.t[h, :, tsl])
                    elif c == 1:
                        for g in range(2):
                            gqa_block(wch, g * 128, gk, kT, kT.t[g, :, tsl])
                        v_block(wch, 256, 256, 0)
                    elif c == 2:
                        for h in range(4):
                            plain_block(wch, h * 128, 64, qT, None, halves=[(0, qT.t[4 + h, :, tsl]), (1, qT.t[20 + h, :, tsl])])
                    elif c == 3:
                        for h in range(4):
                            plain_block(wch, h * 128, 64, kT, kT.t[2 + h, :, tsl])
                    elif c == 4:
                        v_block(wch, 0, 512, 256)
                    elif c == 5:
                        for h in range(4):
                            plain_block(wch, h * 128, 128, qT, qT.t[8 + h, :, tsl])
                    elif c == 6:
                        for g in range(2):
                            plain_block(wch, g * 128, 128, kT, kT.t[6 + g, :, tsl])
                        v_block(wch, 256, 256, 768)
                    elif c == 7:
                        norm_fm(wch, 4, 512, mqn, cq, cqn)
                        for h in range(4):
                            up_plain(wuq, h * 192, 4, cqn, qT, qT.t[12 + h, :, tsl])
                            pacc = proj_fm(wuq, lambda kc: wuq.t[:, kc, h * 192 + 64:h * 192 + 192], 4, cqn,
                                           lambda kc: cqn.t[:, kc, 0:n], 128)
                            xb = xbR.next()
                            S.op("act", lambda e: e.activation(out=xb.t[:, 0:n], in_=pacc.t[:, 0:n], func=AF.Copy),
                                 reads=[pacc], writes=[xb])
                            rope_store(xb, 128, 64, qT, qT.t[16 + h, :, tsl])
                    else:
                        norm_fm(wch, 2, 256, mkn, cq, cqn)
                        for h in range(4):
                            up_plain(wukv, h * 256, 2, cqn, kT, kT.t[8 + h, :, tsl])
                        for s in range(n // 128):
                            pacc, vst = paccR.next(), vstR.next()
                            for h in range(4):
                                for kc in range(2):
                                    S.op("pe", lambda e: e.matmul(pacc.t[:, h * 128:(h + 1) * 128], lhsT=cqn.t[:, kc, s * 128:(s + 1) * 128],
                                                                  rhs=wukv.t[:, kc, h * 256 + 128:h * 256 + 256],
                                                                  start=(kc == 0), stop=(kc == 1)),
                                         reads=[wukv, cqn], writes=[pacc], sig=(h == 3 and kc == 1))
                            S.op("act", lambda e: e.activation(out=vst.t[:, 0:512], in_=pacc.t[:, 0:512], func=AF.Copy),
                                 reads=[pacc], writes=[vst])
                            store(vv, vv.t[tok0 + s * 128:tok0 + (s + 1) * 128, 1024:1536], vst, vst.t[:, 0:512])
                        plain_block(wch, 192, 64, kT, None, halves=[(1, kT.t[12, :, tsl])])

    def phaseB(self, l):
        nc, S = self.nc, self.S
        qT, kT, vv, aT = self.qT[l], self.kT[l], self.vv[l], self.aT[l]
        NTOK, NCH, NT = self.NTOK, self.NCH, self.NT
        LCH = self.Tl // 128
        lam_init = 0.8 - 0.6 * math.exp(-0.3 * l)
        with ExitStack() as es:
            al = lambda name, shape, dt: T(es.enter_context(nc.sbuf_tensor(self.uname(name), list(shape), dt)))
            pal = lambda name, shape, dt: T(es.enter_context(nc.psum_tensor(self.uname(name), list(shape), dt)))
            mask = al("mask", (128, 6, 512), BF16)
            S.dma("pool", mask.t[:], self.c_mask.rearrange("p (d n) -> p d n", d=6), writes=[mask])
            stR = Ring([pal(f"pst{i}", (128, 512), F32) for i in range(4)])
            lq = al("lq", (128, 4, 64), F32)
            lrow = al("lrow", (1, 260), F32)
            for i, nm in enumerate(["diff_lambda_q1", "diff_lambda_k1", "diff_lambda_q2", "diff_lambda_k2"]):
                S.dma("sp", lrow.t[0:1, i * 64:(i + 1) * 64], self.sm[nm][l:l + 1, :], writes=[lrow], acc=(i > 0))
            S.dma("sp", lrow.t[0:1, 256:260], self.sm["win_sinks"][l:l + 1, :], writes=[lrow], acc=True)
            pbc = stR.next()
            S.op("pe", lambda e: e.matmul(pbc.t[:, 0:260], lhsT=self.onesf.t[0:1, :], rhs=lrow.t[0:1, :], start=True, stop=True),
                 reads=[self.onesf, lrow], writes=[pbc])
            S.op("dve", lambda e: e.tensor_copy(out=lq.t[:].rearrange("p a b -> p (a b)"), in_=pbc.t[:, 0:256]), reads=[pbc], writes=[lq])
            lp = al("lp", (128, 2, 64), F32)
            sml = al("sml", (128, 8), F32)
            S.op("dve", lambda e: e.tensor_tensor(out=lp.t[:, 0, :], in0=lq.t[:, 0, :], in1=lq.t[:, 1, :], op=ALU.mult), reads=[lq], writes=[lp])
            S.op("dve", lambda e: e.tensor_tensor(out=lp.t[:, 1, :], in0=lq.t[:, 2, :], in1=lq.t[:, 3, :], op=ALU.mult), reads=[lq], writes=[lp])
            S.op("dve", lambda e: e.memset(sml.t[:], 0.0), writes=[sml])
            S.op("dve", lambda e: e.reduce_sum(out=sml.t[:, 0:2], in_=lp.t[:], axis=AX.X), reads=[lp], writes=[sml])
            S.op("act", lambda e: e.activation(out=sml.t[:, 2:4], in_=sml.t[:, 0:2], func=AF.Exp), reads=[sml], writes=[sml])
            S.op("dve", lambda e: e.tensor_tensor(out=sml.t[:, 4:5], in0=sml.t[:, 2:3], in1=sml.t[:, 3:4], op=ALU.subtract), reads=[sml], writes=[sml])
            S.op("dve", lambda e: e.tensor_scalar(out=sml.t[:, 5:6], in0=sml.t[:, 4:5], scalar1=-1.0, scalar2=-lam_init,
                                                  op0=ALU.mult, op1=ALU.add), reads=[sml], writes=[sml])
            subg = al("subg", (128, 2), F32)
            S.dma("sp", subg.t[:, 0:2], self.abd[l].t[:, 136:138], reads=[self.abd[l]], writes=[subg])
            S.op("dve", lambda e: e.tensor_scalar(out=subg.t[:, 1:2], in0=subg.t[:, 0:1], scalar1=1.0 - lam_init, scalar2=0.0,
                                                  op0=ALU.mult, op1=ALU.add), reads=[subg], writes=[subg])
            snk = al("snk", (128, 8), F32)
            S.op("dve", lambda e: e.tensor_copy(out=snk.t[:, 0:4], in_=pbc.t[:, 256:260]), reads=[pbc], writes=[snk])
            S.op("act", lambda e: e.activation(out=snk.t[:, 4:8], in_=snk.t[:, 0:4], func=AF.Exp), reads=[snk], writes=[snk])

            KTR = Ring([al(f"KT{i}", (128, NTOK), BF16) for i in range(2)])
            VVR = Ring([al(f"VV{i}", (128, NCH, 128), BF16) for i in range(2)])
            KRt = al("KR", (128, NTOK), BF16)
            QTR = Ring([al(f"QT{i}", (128, 512), BF16) for i in range(3)])
            QRR = Ring([al(f"QR{i}", (128, 512), BF16) for i in range(3)])
            PTR = Ring([al(f"PT{i}", (128, 512), BF16) for i in range(4)])
            PMR = Ring([al(f"PM{i}", (128, 512), BF16) for i in range(2)])
            mk = lambda nm, k, dt: Ring([al(f"{nm}{i}", (128, 512), dt) for i in range(k)])
            rinvR, ft1R, ft2R, fdR, fsqR, fstdR, frstdR, fobR = (mk("rinv", 3, F32), mk("ft1", 2, F32), mk("ft2", 2, F32),
                                                                  mk("fd", 2, F32), mk("fsq", 2, BF16), mk("fstd", 2, F32),
                                                                  mk("frstd", 2, F32), mk("fob", 3, BF16))
            accs = [pal(f"pac{i}", (128, 512), F32) for i in range(4)]
            accsel = [0]

            def qtiles(kind):
                out = []
                for j in range(NT):
                    if kind == "win":
                        ch = [(LCH, None), (LCH + 1, None)]
                        for c in range(4 * j - 1, 4 * j + 5):
                            if 0 <= c < LCH:
                                ch.append((c, c - 4 * j + 1))
                    else:
                        ch = [(c, None) for c in range(NCH)]
                    out.append((j * 512, 512, ch))
                if l < self.nlayers - 1 or self.debug:
                    out.append((self.Tl, CTX, [(LCH, None), (LCH + 1, None)]))
                return out

            def load_kv(kidx, vcol0):
                KT, VV = KTR.next(), VVR.next()
                S.dma("sp", KT.t[:], kT.t[kidx, :, :], reads=[kT], writes=[KT])
                S.dma("sp", VV.t[:], vv.t[:, vcol0:vcol0 + 128].rearrange("(c p) e -> p c e", p=128), reads=[vv], writes=[VV])
                return KT, VV

            def attend(kind, h, KT, VV, qidx, oblk, scale):
                nstream = 2 if kind == "diff" else 1
                for (tok0, n, chunks) in qtiles(kind):
                    QT = QTR.next()
                    S.dma("sp", QT.t[:, 0:n], qT.t[qidx, :, tok0:tok0 + n], reads=[qT], writes=[QT])
                    if kind in ("mla", "diff"):
                        QR = QRR.next()
                        S.dma("sp", QR.t[:, 0:n], qT.t[(16 if kind == "mla" else 20) + h, :, tok0:tok0 + n], reads=[qT], writes=[QR])
                    if kind == "diff":
                        myacc = [(accs[0], accs[1]), (accs[2], accs[3])]
                    else:
                        a = accsel[0]
                        accsel[0] = 1 - a
                        myacc = [(accs[2 * a], accs[2 * a + 1])]
                    last = len(chunks) - 1
                    for ci, (c, mi) in enumerate(chunks):
                        cs = slice(c * 128, (c + 1) * 128)
                        for i in range(nstream):
                            st, pt = stR.next(), PTR.next()
                            if kind == "diff":
                                Qi = QT if i == 0 else QR
                                S.op("pe", lambda e: e.matmul(st.t[:, 0:n], lhsT=KT.t[:, cs], rhs=Qi.t[:, 0:n], start=True, stop=True),
                                     reads=[KT, Qi], writes=[st])
                            elif kind == "mla":
                                S.op("pe", lambda e: e.matmul(st.t[:, 0:n], lhsT=KT.t[:, cs], rhs=QT.t[:, 0:n], start=True, stop=False),
                                     reads=[KT, QT], writes=[st], sig=False)
                                S.op("pe", lambda e: e.matmul(st.t[:, 0:n], lhsT=KRt.t[:, cs], rhs=QR.t[:, 0:n], start=False, stop=True),
                                     reads=[KRt, QR], writes=[st])
                            else:
                                S.op("pe", lambda e: e.matmul(st.t[:, 0:n], lhsT=KT.t[:, cs], rhs=QT.t[:, 0:n], start=True, stop=True),
                                     reads=[KT, QT], writes=[st])
                            S.op("act", lambda e: e.activation(out=pt.t[:, 0:n], in_=st.t[:, 0:n], func=AF.Exp, scale=scale),
                                 reads=[st], writes=[pt])
                            if mi is not None:
                                pm = PMR.next()
                                S.op("dve", lambda e: e.tensor_tensor(out=pm.t[:, 0:n], in0=pt.t[:, 0:n], in1=mask.t[:, mi, 0:n], op=ALU.mult),
                                     reads=[pt, mask], writes=[pm])
                                pt = pm
                            O, Sm = myacc[i]
                            S.op("pe", lambda e: e.matmul(O.t[:, 0:n], lhsT=VV.t[:, c, :], rhs=pt.t[:, 0:n], start=(ci == 0), stop=(ci == last)),
                                 reads=[VV, pt], writes=[O], sig=False)
                            S.op("pe", lambda e: e.matmul(Sm.t[:, 0:n], lhsT=self.ones, rhs=pt.t[:, 0:n], start=(ci == 0), stop=(ci == last)),
                                 reads=[self.cmb, pt], writes=[Sm])
                    ob = fobR.next()
                    if kind != "diff":
                        O, Sm = myacc[0]
                        rinv = rinvR.next()
                        if kind == "win":
                            S.op("dve", lambda e: e.tensor_scalar_add(out=rinv.t[:, 0:n], in0=Sm.t[:, 0:n], scalar1=snk.t[:, 4 + h:5 + h]),
                                 reads=[Sm, snk], writes=[rinv])
                            S.op("dve", lambda e: e.reciprocal(out=rinv.t[:, 0:n], in_=rinv.t[:, 0:n]), reads=[rinv], writes=[rinv])
                        else:
                            S.op("dve", lambda e: e.reciprocal(out=rinv.t[:, 0:n], in_=Sm.t[:, 0:n]), reads=[Sm], writes=[rinv])
                        S.op("dve", lambda e: e.tensor_tensor(out=ob.t[:, 0:n], in0=O.t[:, 0:n], in1=rinv.t[:, 0:n], op=ALU.mult),
                             reads=[O, rinv], writes=[ob])
                    else:
                        ts_ = []
                        for i in range(2):
                            O, Sm = myacc[i]
                            rinv, ft = rinvR.next(), (ft1R if i == 0 else ft2R).next()
                            S.op("dve", lambda e: e.reciprocal(out=rinv.t[:, 0:n], in_=Sm.t[:, 0:n]), reads=[Sm], writes=[rinv])
                            S.op("dve", lambda e: e.tensor_tensor(out=ft.t[:, 0:n], in0=O.t[:, 0:n], in1=rinv.t[:, 0:n], op=ALU.mult),
                                 reads=[O, rinv], writes=[ft])
                            ts_.append(ft)
                        fd, fsq, fstd, frstd, paux = fdR.next(), fsqR.next(), fstdR.next(), frstdR.next(), stR.next()
                        S.op("dve", lambda e: e.scalar_tensor_tensor(out=fd.t[:, 0:n], in0=ts_[1].t[:, 0:n], scalar=sml.t[:, 5:6],
                                                                     in1=ts_[0].t[:, 0:n], op0=ALU.mult, op1=ALU.add),
                             reads=[ts_[0], ts_[1], sml], writes=[fd])
                        S.op("act", lambda e: e.activation(out=fsq.t[:, 0:n], in_=fd.t[:, 0:n], func=AF.Square), reads=[fd], writes=[fsq])
                        S.op("pe", lambda e: e.matmul(paux.t[:, 0:n], lhsT=self.ones, rhs=fsq.t[:, 0:n], start=True, stop=True),
                             reads=[fsq, self.cmb], writes=[paux])
                        S.op("act", lambda e: e.activation(out=fstd.t[:, 0:n], in_=paux.t[:, 0:n], func=AF.Sqrt, scale=1.0 / 128, bias=EPS),
                             reads=[paux], writes=[fstd])
                        S.op("dve", lambda e: e.reciprocal(out=frstd.t[:, 0:n], in_=fstd.t[:, 0:n]), reads=[fstd], writes=[frstd])
                        S.op("dve", lambda e: e.scalar_tensor_tensor(out=ob.t[:, 0:n], in0=fd.t[:, 0:n], scalar=subg.t[:, 1:2],
                                                                     in1=frstd.t[:, 0:n], op0=ALU.mult, op1=ALU.mult),
                             reads=[fd, subg, frstd], writes=[ob])
                    S.dma("pool", aT.t[oblk * 128:(oblk + 1) * 128, tok0:tok0 + n], ob.t[:, 0:n], reads=[ob], writes=[aT], acc=True)

            for g in range(2):
                KT, VV = load_kv(g, g * 128)
                for r in range(2):
                    h = 2 * g + r
                    attend("gqa", h, KT, VV, h, h, 128 ** -0.5)
            for h in range(4):
                KT, VV = load_kv(2 + h, 256 + h * 128)
                attend("diff", h, KT, VV, 4 + h, 4 + h, 64 ** -0.5)
            for g in range(2):
                KT, VV = load_kv(6 + g, 768 + g * 128)
                for r in range(2):
                    h = 2 * g + r
                    attend("win", h, KT, VV, 8 + h, 8 + h, 128 ** -0.5)
            S.dma("sp", KRt.t[:], kT.t[12, :, :], reads=[kT], writes=[KRt])
            for h in range(4):
                KT, VV = load_kv(8 + h, 1024 + h * 128)
                attend("mla", h, KT, VV, 12 + h, 12 + h, 192 ** -0.5)

    def phaseC(self, l):
        nc, S = self.nc, self.S
        wb = self.wb[l]
        aT = self.aT[l]
        lastl = (l == self.nlayers - 1)
        with ExitStack() as es:
            al = lambda name, shape, dt: T(es.enter_context(nc.sbuf_tensor(self.uname(name), list(shape), dt)))
            pal = lambda name, shape, dt: T(es.enter_context(nc.psum_tensor(self.uname(name), list(shape), dt)))
            AB = al("AB2", (128, 2, 16, 2), F32)
            S.dma("sp", AB.t[:], self.abd[l].t[:, 64:128].rearrange("p (j k r) -> p j k r", j=2, r=2), reads=[self.abd[l]], writes=[AB])
            Gm = al("Gm", (128, D), F32)
            Gf = al("Gf", (128, D), F32)
            grow = al("growc", (1, D), F32)
            H = al("H", (128, 16, 512), BF16)
            Q = [al(f"Q{i}", (128, D), F32) for i in range(4)]
            xsR = Ring([al(f"xsc{i}", (128, D), F32) for i in range(2)])
            xn = al("xnc", (128, D), BF16)
            actT = al("actT", (128, 44, 512), BF16)
            wpR = Ring([al(f"wp{i}", (128, 16, 512), BF16) for i in range(4)])
            sgR = Ring([al(f"sg{i}", (128, 512), F32) for i in range(2)])
            stR = Ring([al(f"stc{i}", (128, 8), F32) for i in range(4)])
            pb = [pal(f"pc{i}", (128, 512), F32) for i in range(6)]
            ptrR = Ring([pal(f"ptc{i}", (128, 1024), BF16) for i in range(2)])
            pbR = Ring(pb[0:4])
            allps = pb + ptrR.items

            tiles = self.tiles(l, with_ctx=not lastl)
            cur_row = None
            for (tok0, n, is_ctx) in tiles:
                row = 1 if is_ctx else 0
                ns = n // 128
                if cur_row != row:
                    for (Gt, idx) in ((Gm, 2), (Gf, 5)):
                        S.dma("sp", grow.t[0:1, :], self.derd[l].t[row:row + 1, idx, :], reads=[self.derd[l]], writes=[grow])
                        for nb in range(4):
                            pc = pbR.next()
                            S.op("pe", lambda e: e.matmul(pc.t[:], lhsT=self.onesf.t[0:1, :], rhs=grow.t[0:1, nb * 512:(nb + 1) * 512],
                                                          start=True, stop=True), reads=[self.onesf, grow], writes=[pc])
                            S.op("dve", lambda e: e.tensor_copy(out=Gt.t[:, nb * 512:(nb + 1) * 512], in_=pc.t[:]), reads=[pc], writes=[Gt])
                    cur_row = row
                S.dma("sp", H.t[:, :, 0:n], aT.t[:, tok0:tok0 + n].rearrange("(kc p) t -> p kc t", p=128), reads=[aT], writes=[H])
                for nb in range(4):
                    wp = wpR.next()
                    S.dma("sp", wp.t[:], wb["w_out"].t[:, nb * 512:(nb + 1) * 512].rearrange("(kc p) n -> p kc n", p=128),
                          reads=[wb["w_out"]], writes=[wp])
                    for s in range(ns):
                        pc = pbR.next()
                        for kc in range(16):
                            S.op("pe", lambda e: e.matmul(pc.t[:], lhsT=H.t[:, kc, s * 128:(s + 1) * 128], rhs=wp.t[:, kc, :],
                                                          start=(kc == 0), stop=(kc == 15)), reads=[H, wp], writes=[pc], sig=(kc == 15))
                        S.op("act", lambda e: e.activation(out=Q[s].t[:, nb * 512:(nb + 1) * 512], in_=pc.t[:], func=AF.Copy),
                             reads=[pc], writes=[Q[s]])
                src, r0, srcT = self.xsrc(l, tok0, is_ctx)
                for s in range(ns):
                    xs, st = xsR.next(), stR.next()
                    S.dma("sp", xs.t[:], src[r0 + s * 128:r0 + (s + 1) * 128, :], reads=[srcT] if srcT else [], writes=[xs])
                    self.norm_stats(Q[s].t[:], Q[s], xn, st, D)
                    S.op("dve", lambda e: e.scalar_tensor_tensor(out=Q[s].t[:], in0=Q[s].t[:], scalar=st.t[:, 2:3], in1=Gm.t[:],
                                                                 op0=ALU.mult, op1=ALU.mult), reads=[Q[s], st, Gm], writes=[Q[s]])
                    S.op("dve", lambda e: e.tensor_tensor(out=Q[s].t[:], in0=Q[s].t[:], in1=xs.t[:], op=ALU.add),
                         reads=[Q[s], xs], writes=[Q[s]])
                    st2 = stR.next()
                    self.norm_stats(Q[s].t[:], Q[s], xn, st2, D)
                    self.to_hT(Q[s], st2, xn, ptrR, H, s, AB, row)
                for fg in range(11):
                    wg, wu = wpR.next(), wpR.next()
                    S.dma("sp", wg.t[:], wb["ffn_w_gate"].t[:, fg * 512:(fg + 1) * 512].rearrange("(kc p) n -> p kc n", p=128),
                          reads=[wb["ffn_w_gate"]], writes=[wg])
                    S.dma("sp", wu.t[:], wb["ffn_w_up"].t[:, fg * 512:(fg + 1) * 512].rearrange("(kc p) n -> p kc n", p=128),
                          reads=[wb["ffn_w_up"]], writes=[wu])
                    for j in range(4):
                        fb = fg * 4 + j
                        pg, pu = pbR.next(), pbR.next()
                        for kc in range(16):
                            S.op("pe", lambda e: e.matmul(pg.t[:, 0:n], lhsT=wg.t[:, kc, j * 128:(j + 1) * 128], rhs=H.t[:, kc, 0:n],
                                                          start=(kc == 0), stop=(kc == 15)), reads=[wg, H], writes=[pg], sig=(kc == 15))
                        for kc in range(16):
                            S.op("pe", lambda e: e.matmul(pu.t[:, 0:n], lhsT=wu.t[:, kc, j * 128:(j + 1) * 128], rhs=H.t[:, kc, 0:n],
                                                          start=(kc == 0), stop=(kc == 15)), reads=[wu, H], writes=[pu], sig=(kc == 15))
                        sg = sgR.next()
                        S.op("act", lambda e: e.activation(out=sg.t[:, 0:n], in_=pg.t[:, 0:n], func=AF.Silu), reads=[pg], writes=[sg])
                        S.op("dve", lambda e: e.tensor_tensor(out=actT.t[:, fb, 0:n], in0=sg.t[:, 0:n], in1=pu.t[:, 0:n], op=ALU.mult),
                             reads=[sg, pu], writes=[actT])
                dst = self.out if lastl else self.x1.t
                dstT = None if lastl else self.x1
                for sp0 in range(0, ns, 2):
                    for nb in range(4):
                        for qd in range(4):
                            wp = wpR.next()
                            S.dma("sp", wp.t[:, 0:11, :],
                                  wb["ffn_w_down"].t[qd * 1408:(qd + 1) * 1408, nb * 512:(nb + 1) * 512].rearrange("(kc p) n -> p kc n", p=128),
                                  reads=[wb["ffn_w_down"]], writes=[wp])
                            for si in range(2):
                                s = sp0 + si
                                bank = allps[si * 4 + nb]
                                bap = bank.t[:] if bank.t.dtype == F32 else bank.t[:].bitcast(F32)
                                for i in range(11):
                                    fc = qd * 11 + i
                                    S.op("pe", lambda e: e.matmul(bap, lhsT=actT.t[:, fc, s * 128:(s + 1) * 128], rhs=wp.t[:, i, :],
                                                                  start=(fc == 0), stop=(fc == 43)), reads=[actT, wp], writes=[bank],
                                         sig=(i == 10))
                    for si in range(2):
                        s = sp0 + si
                        st = stR.next()
                        S.op("dve", lambda e: e.memset(st.t[:], 0.0), writes=[st])
                        for nb in range(4):
                            bank = allps[si * 4 + nb]
                            bap = bank.t[:] if bank.t.dtype == F32 else bank.t[:].bitcast(F32)
                            S.op("act", lambda e: e.activation(out=xn.t[:, 0:512], in_=bap, func=AF.Square, accum_out=st.t[:, 4 + nb:5 + nb]),
                                 reads=[bank, st], writes=[xn, st])
                        S.op("dve", lambda e: e.reduce_sum(out=st.t[:, 0:1], in_=st.t[:, 4:8], axis=AX.X), reads=[st], writes=[st])
                        S.op("act", lambda e: e.activation(out=st.t[:, 1:2], in_=st.t[:, 0:1], func=AF.Sqrt, scale=1.0 / D, bias=EPS),
                             reads=[st], writes=[st])
                        S.op("dve", lambda e: e.reciprocal(out=st.t[:, 2:3], in_=st.t[:, 1:2]), reads=[st], writes=[st])
                        xs = xsR.next()
                        for nb in range(4):
                            bank = allps[si * 4 + nb]
                            bap = bank.t[:] if bank.t.dtype == F32 else bank.t[:].bitcast(F32)
                            cs = slice(nb * 512, (nb + 1) * 512)
                            S.op("dve", lambda e: e.scalar_tensor_tensor(out=xs.t[:, cs], in0=bap, scalar=st.t[:, 2:3], in1=Gf.t[:, cs],
                                                                         op0=ALU.mult, op1=ALU.mult), reads=[bank, st, Gf], writes=[xs])
                        S.op("dve", lambda e: e.tensor_tensor(out=xs.t[:], in0=xs.t[:], in1=Q[s].t[:], op=ALU.add),
                             reads=[xs, Q[s]], writes=[xs])
                        S.dma("pool", dst[tok0 + s * 128:tok0 + (s + 1) * 128, :], xs.t[:], reads=[xs],
                              writes=[dstT] if dstT else [], acc=True)


_CACHE = {}


def _get_nc(NT, nlayers=2, debug=False, G=1):
    key = (NT, nlayers, debug, G)
    if key not in _CACHE:
        _CACHE[key] = Builder(NT, nlayers, debug, G).build()
    return _CACHE[key]


def kernel(x, c, ctx, c_ctx, ada_w, ada_b, norm_pre_mix, norm_post_mix, norm_pre_ffn, norm_post_ffn,
           w_in, gqa_q_norm, gqa_k_norm, diff_lambda_q1, diff_lambda_k1, diff_lambda_q2, diff_lambda_k2,
           diff_subln, win_sinks, mla_q_norm, mla_w_uq, mla_kv_norm, mla_w_ukv, w_out,
           ffn_w_gate, ffn_w_up, ffn_w_down, _nlayers=2, _debug=False):
    f = lambda a: np.ascontiguousarray(np.asarray(a, dtype=np.float32))
    x = f(x)
    B, Tl, _ = x.shape
    NT = Tl // 512
    nc = _get_nc(NT, _nlayers, _debug, 1)
    consts = host_consts(Tl)
    big = {"ada_w": f(ada_w), "w_in": f(w_in), "mla_w_uq": f(mla_w_uq), "mla_w_ukv": f(mla_w_ukv), "w_out": f(w_out),
           "ffn_w_gate": f(ffn_w_gate), "ffn_w_up": f(ffn_w_up), "ffn_w_down": f(ffn_w_down)}
    shared = {"ada_b": f(ada_b), "norm_pre_mix": f(norm_pre_mix), "norm_post_mix": f(norm_post_mix),
              "norm_pre_ffn": f(norm_pre_ffn), "norm_post_ffn": f(norm_post_ffn), "gqa_q_norm": f(gqa_q_norm),
              "gqa_k_norm": f(gqa_k_norm), "diff_lambda_q1": f(diff_lambda_q1), "diff_lambda_k1": f(diff_lambda_k1),
              "diff_lambda_q2": f(diff_lambda_q2), "diff_lambda_k2": f(diff_lambda_k2), "diff_subln": f(diff_subln),
              "win_sinks": f(win_sinks), "mla_q_norm": f(mla_q_norm), "mla_kv_norm": f(mla_kv_norm)}
    shared.update(consts)
    c = f(c)
    ctx = f(ctx)
    c_ctx = f(c_ctx)
    in_maps = []
    for b in range(B):
        m = dict(shared)
        for k, v in big.items():
            m[k] = v
        m["x"] = x[b]
        m["ctx"] = ctx[b]
        m["cc"] = np.ascontiguousarray(np.stack([c[b], c_ctx], axis=0))
        in_maps.append(m)
    res = run_bass_kernel_spmd(nc, in_maps, core_ids=list(range(B)))
    return np.stack([np.asarray(r["out"], dtype=np.float32) for r in res.results], axis=0)
```
